# Optimizing a Trainium2 kernel written in Bass

```python
import jax, jax.numpy as jnp
from jax import lax
import numpy as np

D_MODEL = 2048
BATCH = 4
SEQ = 4096
DEPTH = 2

N_META = 16
BLOCK = 128
PAD = BLOCK - N_META
EPS = 1e-6
MASK_VALUE = -1e30

FOX_HEADS = 8
FOX_HEAD_DIM = D_MODEL // 16
FOX_WIDTH = FOX_HEADS * FOX_HEAD_DIM

GLA_HEADS = 4
GLA_KEY_WIDTH = D_MODEL // 2
GLA_VAL_WIDTH = D_MODEL
GLA_DK = GLA_KEY_WIDTH // GLA_HEADS
GLA_DV = GLA_VAL_WIDTH // GLA_HEADS
GLA_RANK = 16
GLA_TAU = 16.0
GLA_CHUNK = 64

D_FF = 4 * D_MODEL

SPLITS = (FOX_WIDTH, FOX_WIDTH, FOX_WIDTH, FOX_HEADS,
          GLA_KEY_WIDTH, GLA_KEY_WIDTH, GLA_VAL_WIDTH, GLA_VAL_WIDTH, GLA_RANK,
          D_MODEL, D_MODEL)
D_IN_PROJ = 3 * FOX_WIDTH + FOX_HEADS + 2 * GLA_KEY_WIDTH + 2 * GLA_VAL_WIDTH + GLA_RANK + 2 * D_MODEL

kernel_name = "fox_gla_gated_hybrid_block"


def rmsnorm(x, g):
    xf = x.astype(jnp.float32)
    y = xf * lax.rsqrt(jnp.mean(xf * xf, axis=-1, keepdims=True) + EPS)
    return (y * g.astype(jnp.float32)).astype(x.dtype)


def split_heads(t, h):
    b, l, _ = t.shape
    return t.reshape(b, l, h, -1).transpose(0, 2, 1, 3)


def forgetting_attention(q, k, v, f_logit, b_forget):
    B, L, _ = q.shape
    Lp = L + PAD
    pad4 = ((0, 0), (0, 0), (PAD, 0), (0, 0))
    qh = jnp.pad(split_heads(q, FOX_HEADS), pad4)
    kh = jnp.pad(split_heads(k, FOX_HEADS), pad4)
    vh = jnp.pad(split_heads(v, FOX_HEADS), pad4)
    log_f = jax.nn.log_sigmoid((f_logit + b_forget).astype(jnp.float32))
    log_f = jnp.pad(log_f.transpose(0, 2, 1), ((0, 0), (0, 0), (PAD, 0)))
    c = jnp.cumsum(log_f, axis=-1)
    scale = FOX_HEAD_DIM ** -0.5
    pos = np.arange(Lp)
    outs = []
    for i in range(Lp // BLOCK):
        q0, q1 = i * BLOCK, (i + 1) * BLOCK
        s = jnp.einsum('bhqd,bhkd->bhqk', qh[:, :, q0:q1], kh[:, :, :q1],
                       preferred_element_type=jnp.float32) * scale
        s = s + c[:, :, q0:q1, None] - c[:, :, None, :q1]
        mask = (pos[None, :q1] <= pos[q0:q1, None]) & (pos[None, :q1] >= PAD)
        s = jnp.where(mask, s, MASK_VALUE)
        p = jax.nn.softmax(s, axis=-1)
        outs.append(jnp.einsum('bhqk,bhkd->bhqd', p.astype(vh.dtype), vh[:, :, :q1]))
    o = jnp.concatenate(outs, axis=2)[:, :, PAD:]
    return o.transpose(0, 2, 1, 3).reshape(B, L, FOX_WIDTH)


def gated_linear_attention(q, k, v, g):
    B, L, _ = q.shape
    Lp = L + PAD
    N, C = Lp // GLA_CHUNK, GLA_CHUNK

    def chunks(t):
        t = jnp.pad(t.astype(jnp.float32), ((0, 0), (PAD, 0), (0, 0)))
        return t.reshape(B, N, C, GLA_HEADS, -1).transpose(0, 3, 1, 2, 4)

    qc = chunks(q) * (GLA_DK ** -0.5)
    kc, vc, gc = chunks(k), chunks(v), chunks(g)
    b = jnp.cumsum(gc, axis=3)
    b_last = b[:, :, :, -1:]
    q_dec = qc * jnp.exp(b)
    k_inv = kc * jnp.exp(-b)
    k_end = kc * jnp.exp(b_last - b)
    causal = np.tril(np.ones((C, C), dtype=bool))
    a = jnp.where(causal, jnp.einsum('bhnck,bhnsk->bhncs', q_dec, k_inv), 0.0)
    o_intra = jnp.einsum('bhncs,bhnsv->bhncv', a, vc)

    def step(S, inp):
        qd, ke, vv, dl = inp
        o = jnp.einsum('bhck,bhkv->bhcv', qd, S)
        S = S * dl[..., None] + jnp.einsum('bhck,bhcv->bhkv', ke, vv)
        return S, o

    S0 = jnp.zeros((B, GLA_HEADS, GLA_DK, GLA_DV), jnp.float32)
    xs = (jnp.moveaxis(q_dec, 2, 0), jnp.moveaxis(k_end, 2, 0), jnp.moveaxis(vc, 2, 0),
          jnp.moveaxis(jnp.exp(b_last[:, :, :, 0]), 2, 0))
    _, o_inter = lax.scan(step, S0, xs)
    o = o_intra + jnp.moveaxis(o_inter, 0, 2)
    return o.transpose(0, 2, 3, 1, 4).reshape(B, Lp, GLA_VAL_WIDTH)[:, PAD:]


def head_rmsnorm(o, g):
    B, L, _ = o.shape
    oh = o.reshape(B, L, GLA_HEADS, GLA_DV)
    oh = oh * lax.rsqrt(jnp.mean(oh * oh, axis=-1, keepdims=True) + EPS)
    return oh.reshape(B, L, GLA_VAL_WIDTH) * g.astype(jnp.float32)


def hybrid_mixer(xn, w_in, b_forget, w_alpha2, b_alpha, gla_norm_g, w_o_fox, w_o_gla, w_out):
    proj = xn @ w_in
    split_points = np.cumsum(SPLITS)[:-1].tolist()
    (fq, fk, fv, ff, gq, gk, gv, gr, ga, gate_fox, gate_gla) = jnp.split(proj, split_points, axis=-1)
    o_fox = forgetting_attention(fq, fk, fv, ff, b_forget)
    g_log = jax.nn.log_sigmoid((ga @ w_alpha2 + b_alpha).astype(jnp.float32)) / GLA_TAU
    o_gla = gated_linear_attention(gq, gk, gv, g_log)
    o_gla = (head_rmsnorm(o_gla, gla_norm_g) * jax.nn.silu(gr.astype(jnp.float32))).astype(xn.dtype)
    y = jax.nn.sigmoid(gate_fox) * (o_fox @ w_o_fox) + jax.nn.sigmoid(gate_gla) * (o_gla @ w_o_gla)
    return y @ w_out


def squared_relu_mlp(xn, w1, w2):
    return jnp.square(jax.nn.relu(xn @ w1)) @ w2


def setup_inputs(seed: int = 0) -> dict:
    key = jax.random.key(seed)
    ks = jax.random.split(key, 16)
    nrm = lambda k, shape, fan_in: jax.random.normal(k, shape, jnp.float32) * (fan_in ** -0.5)
    gain = lambda k, shape: 1.0 + 0.02 * jax.random.normal(k, shape, jnp.float32)
    return {
        "x": jax.random.normal(ks[0], (BATCH, SEQ, D_MODEL), jnp.float32),
        "meta_tokens": jax.random.normal(ks[1], (N_META, D_MODEL), jnp.float32),
        "norm_mix_g": gain(ks[2], (DEPTH, D_MODEL)),
        "w_in": nrm(ks[3], (DEPTH, D_MODEL, D_IN_PROJ), D_MODEL),
        "b_forget": jax.random.uniform(ks[4], (DEPTH, FOX_HEADS), jnp.float32, 1.0, 5.0),
        "w_alpha2": nrm(ks[5], (DEPTH, GLA_RANK, GLA_KEY_WIDTH), GLA_RANK),
        "b_alpha": 0.01 * jax.random.normal(ks[6], (DEPTH, GLA_KEY_WIDTH), jnp.float32),
        "gla_norm_g": gain(ks[7], (DEPTH, GLA_VAL_WIDTH)),
        "w_o_fox": nrm(ks[8], (DEPTH, FOX_WIDTH, D_MODEL), FOX_WIDTH),
        "w_o_gla": nrm(ks[9], (DEPTH, GLA_VAL_WIDTH, D_MODEL), GLA_VAL_WIDTH),
        "w_out": nrm(ks[10], (DEPTH, D_MODEL, D_MODEL), D_MODEL),
        "norm_mlp_g": gain(ks[11], (DEPTH, D_MODEL)),
        "w_ff1": nrm(ks[12], (DEPTH, D_MODEL, D_FF), D_MODEL),
        "w_ff2": nrm(ks[13], (DEPTH, D_FF, D_MODEL), D_FF),
        "final_norm_g": gain(ks[14], (D_MODEL,)),
    }


def reference(x, meta_tokens, norm_mix_g, w_in, b_forget, w_alpha2, b_alpha, gla_norm_g,
              w_o_fox, w_o_gla, w_out, norm_mlp_g, w_ff1, w_ff2, final_norm_g):
    B = x.shape[0]
    meta = jnp.broadcast_to(meta_tokens[None].astype(x.dtype), (B, N_META, D_MODEL))
    h = jnp.concatenate([meta, x], axis=1)
    for l in range(DEPTH):
        h = h + hybrid_mixer(rmsnorm(h, norm_mix_g[l]), w_in[l], b_forget[l], w_alpha2[l],
                             b_alpha[l], gla_norm_g[l], w_o_fox[l], w_o_gla[l], w_out[l])
        h = h + squared_relu_mlp(rmsnorm(h, norm_mlp_g[l]), w_ff1[l], w_ff2[l])
    return rmsnorm(h, final_norm_g)[:, N_META:]
```

```python
import numpy as np
import ml_dtypes
import concourse.bass as bass
import concourse.mybir as mybir
from concourse.bass_utils import run_bass_kernel_spmd

F32 = mybir.dt.float32
BF16 = mybir.dt.bfloat16
AF = mybir.ActivationFunctionType
ALU = mybir.AluOpType

NL = 2
D = 2048
NCH = 16
NBLK = 33
TG = NBLK * 128
PADT = 112
EPS = 1e-6
P1_SB = 11
P1_NS = 4
P1_SW = 352
P1_T = P1_SB * 128
P4_SB = 3
P4_T = P4_SB * 128
NIB = 26

SEM_ROT = 30000
NDMASEM = 12


class Unit:
    __slots__ = ("name", "w", "r")

    def __init__(self, name):
        self.name = name
        self.w = None
        self.r = []


class Op:
    __slots__ = ("eng", "fn", "raw", "oth", "dma", "sig", "sigsem", "sigval", "prev_tok", "grp")

    def __init__(self, eng, fn, dma):
        self.eng = eng
        self.fn = fn
        self.dma = dma
        self.raw = set()
        self.oth = set()
        self.sig = False
        self.sigsem = None
        self.sigval = 0
        self.prev_tok = None
        self.grp = None


class K:
    ENGS = ("sync", "scalar", "gpsimd", "tensor", "vector")

    def __init__(self, nc):
        self.nc = nc
        self.ops = {e: [] for e in self.ENGS}

    def unit(self, name=""):
        return Unit(name)

    def units(self, n, name=""):
        return [Unit(f"{name}{i}") for i in range(n)]

    def op(self, eng, fn, reads=(), writes=(), dma=False):
        o = Op(eng, fn, dma)
        for u in reads:
            if u.w is not None and u.w is not o:
                o.raw.add(u.w)
            u.r.append(o)
        for u in writes:
            if u.w is not None and u.w is not o:
                o.oth.add(u.w)
            for r in u.r:
                if r is not o:
                    o.oth.add(r)
            u.w = o
            u.r = []
        self.ops[eng].append(o)
        return o

    def dma(self, eng, out, in_, reads=(), writes=(), grp=None, **kw):
        o = self.op(eng, lambda e: e.dma_start(out=out, in_=in_, **kw), reads, writes, dma=True)
        if grp is not None:
            o.grp = grp
            grp.append(o)
        return o

    def emit(self, final_wait_ops=()):
        nc = self.nc

        def need(o, p, raw):
            if p.dma or p.eng != o.eng or o.dma:
                return True
            return raw and o.eng != "tensor"

        for e in self.ENGS:
            for o in self.ops[e]:
                for p in o.raw:
                    if need(o, p, True):
                        p.sig = True
                for p in o.oth:
                    if need(o, p, False):
                        p.sig = True
        for o in final_wait_ops:
            o.sig = True
        for e in self.ENGS:
            for o in self.ops[e]:
                if o.dma:
                    o.sig = True
        nsem = [0]

        def newsem(name):
            nsem[0] += 1
            return nc.alloc_semaphore(name)

        for e in self.ENGS:
            cur = None
            cnt = 0
            dsems, dvals, dlast = [], [], []
            di = 0
            for o in self.ops[e]:
                if not o.sig:
                    continue
                if o.dma and o.grp is not None:
                    g = o.grp
                    if not isinstance(g[0], tuple):
                        g.insert(0, ("sem", newsem(f"g_{nsem[0]}")))
                    o.sigsem = g[0][1]
                    o.sigval = 16 * (len(g) - 1)
                elif o.dma:
                    if len(dsems) < NDMASEM:
                        dsems.append(newsem(f"d_{e}_{len(dsems)}"))
                        dvals.append(0)
                        dlast.append(None)
                        j = len(dsems) - 1
                    else:
                        j = di % NDMASEM
                    di += 1
                    if dvals[j] + 16 > 60000:
                        dsems[j] = newsem(f"d_{e}_{j}_r{di}")
                        dvals[j] = 0
                        dlast[j] = None
                    dvals[j] += 16
                    o.sigsem = dsems[j]
                    o.sigval = dvals[j]
                    o.prev_tok = dlast[j]
                    dlast[j] = o
                else:
                    if cur is None or cnt >= SEM_ROT:
                        cur = newsem(f"c_{e}_{nsem[0]}")
                        cnt = 0
                    cnt += 1
                    o.sigsem = cur
                    o.sigval = cnt
        self.nsems = nsem[0]
        with nc.Block() as block:
            def make(ename):
                def body(eng):
                    waited = {}

                    def wait(p):
                        key = id(p.sigsem)
                        if waited.get(key, 0) >= p.sigval:
                            return
                        eng.wait_ge(p.sigsem, p.sigval)
                        waited[key] = p.sigval

                    for o in self.ops[ename]:
                        for p in o.raw:
                            if need(o, p, True):
                                wait(p)
                        for p in o.oth:
                            if need(o, p, False):
                                wait(p)
                        if o.dma and o.prev_tok is not None:
                            wait(o.prev_tok)
                        ins = o.fn(eng)
                        if o.sig:
                            ins.then_inc(o.sigsem, 16 if o.dma else 1)
                    if ename == "sync":
                        for p in final_wait_ops:
                            wait(p)
                return body
            block.sync(make("sync"))
            block.scalar(make("scalar"))
            block.gpsimd(make("gpsimd"))
            block.tensor(make("tensor"))
            block.vector(make("vector"))


def build(nlayers=NL, stop_after=None, dbg=()):
    nc = bass.Bass("TRN2", target_bir_lowering=False)
    k = K(nc)

    def din(name, shape, dt=F32):
        return nc.dram_tensor(name, list(shape), dt, kind="ExternalInput").ap()

    def dscr(name, shape, dt):
        kind = "ExternalOutput" if name in dbg else "Internal"
        return nc.dram_tensor(name, list(shape), dt, kind=kind).ap()

    xT = din("xT", [NCH, 128, TG])
    w_inb = din("w_inb", [NL * NIB * 512, 2048])
    w_sm = din("w_sm", [128, NL, NCH, 24])
    w_yc = din("w_yc", [NL * 8 * 384, 2048])
    w_ot = din("w_ot", [NL * 4 * 512, 2048])
    w_f1 = din("w_f1", [NL * 16 * 512, 2048])
    w_f2 = din("w_f2", [NL * 16 * 512, 2048])
    gvec = din("gvec", [128, 4, NL, NCH])
    bfor = din("bfor", [128, NL, 8])
    waug = din("waug", [33, NL, 1024])
    cst = din("cst", [128, 4, 128])
    cstb = din("cstb", [128, 3, 128], BF16)
    padv = din("padv", [128, 1])
    outT = nc.dram_tensor("outT", [NCH, 128, TG], F32, kind="ExternalOutput").ap()

    b_inb = dscr("b_inb", [NL * NIB * 512, 2048], BF16)
    b_yc = dscr("b_yc", [NL * 8 * 384, 2048], BF16)
    b_ot = dscr("b_ot", [NL * 4 * 512, 2048], BF16)
    b_f1 = dscr("b_f1", [NL * 16 * 512, 2048], BF16)
    b_f2 = dscr("b_f2", [NL * 16 * 512, 2048], BF16)
    hT = dscr("hT", [NCH, 128, TG], F32)
    qT = dscr("qT", [8, 128, TG], BF16)
    kT = dscr("kT", [8, 128, TG], BF16)
    v_tok = dscr("v_tok", [TG, 1024], BF16)
    nlf = dscr("nlf", [TG, 8], F32)
    gqT = dscr("gqT", [8, 128, TG], BF16)
    gkT = dscr("gkT", [8, 128, TG], BF16)
    gv_tok = dscr("gv_tok", [TG, 2048], BF16)
    l_tok = dscr("l_tok", [TG, 1024], F32)
    sgrT = dscr("sgrT", [NCH, 128, TG], BF16)
    sgfT = dscr("sgfT", [NCH, 128, TG], BF16)
    sggT = dscr("sggT", [NCH, 128, TG], BF16)
    ofoxT = dscr("ofoxT", [8, 128, TG], BF16)
    onT = dscr("onT", [NCH, 128, TG], BF16)

    dbg_y = dscr("dbg_y", [NCH, 128, TG], BF16) if "dbg_y" in dbg else None
    dbg_h1 = dscr("dbg_h1", [NCH, 128, TG], F32) if "dbg_h1" in dbg else None
    dbg_a = dscr("dbg_a", [64, 128, TG], BF16) if "dbg_a" in dbg else None
    du = {}

    def U(name, key):
        kk = (name, key)
        if kk not in du:
            du[kk] = Unit(f"{name}{key}")
        return du[kk]

    PS = [nc.alloc_psum_tensor(f"ps{i}", [128, 512], F32) for i in range(8)]
    PSU = [Unit(f"ps{i}") for i in range(8)]

    gv_sb = nc.alloc_sbuf_tensor("gv_sb", [128, 4, NL, NCH], F32)
    cst_sb = nc.alloc_sbuf_tensor("cst_sb", [128, 4, 128], F32)
    cstb_sb = nc.alloc_sbuf_tensor("cstb_sb", [128, 3, 128], BF16)
    padv_sb = nc.alloc_sbuf_tensor("padv_sb", [128, 1], F32)
    eps_sb = nc.alloc_sbuf_tensor("eps_sb", [128, 1], F32)
    bfor_sb = nc.alloc_sbuf_tensor("bfor_sb", [128, NL, 8], F32)
    waug_sb = nc.alloc_sbuf_tensor("waug_sb", [33, NL, 1024], BF16)
    wsm_sb = nc.alloc_sbuf_tensor("wsm_sb", [128, NL, NCH, 24], BF16)
    u_const = Unit("const")
    k.dma("sync", gv_sb[:], gvec, writes=[u_const])
    u_c = [Unit("cc%d" % i) for i in range(6)]
    k.dma("sync", cst_sb[:], cst, writes=[u_c[0]])
    k.dma("sync", cstb_sb[:], cstb, writes=[u_c[1]])
    k.dma("sync", padv_sb[:], padv, writes=[u_c[2]])
    k.dma("sync", bfor_sb[:], bfor, writes=[u_c[3]])
    k.dma("gpsimd", waug_sb[:], waug, writes=[u_c[4]])
    k.dma("gpsimd", wsm_sb[:], w_sm, writes=[u_c[5]])
    u_eps = Unit("eps")
    k.op("gpsimd", lambda e: e.memset(eps_sb[:], EPS), writes=[u_eps])
    CONST = [u_const, u_eps] + u_c
    Uneg = cst_sb[:, 0, :]
    Uneg16 = cst_sb[:, 1, :]
    E63 = cst_sb[:, 2, :]
    E127 = cst_sb[:, 3, :]
    trimask = cstb_sb[:, 0, :]
    ident = cstb_sb[:, 1, :]
    ones = cstb_sb[:, 2, :]

    cast_specs = {"b_inb": (w_inb, b_inb, NIB, 512), "b_yc": (w_yc, b_yc, 8, 384), "b_ot": (w_ot, b_ot, 4, 512),
                  "b_f1": (w_f1, b_f1, 16, 512), "b_f2": (w_f2, b_f2, 16, 512)}
    cast_queue = []
    for l_ in range(nlayers):
        for nm_ in (("b_inb",) if l_ > 0 else ()) + ("b_yc", "b_ot", "b_f1", "b_f2"):
            for i_ in range(cast_specs[nm_][2]):
                cast_queue.append((nm_, l_, i_))
        if l_ == 0 and nlayers > 1:
            pass
    cast_queue.sort(key=lambda t: (t[1], 0 if t[0] == "b_inb" else 1))

    def emit_cast(nm_, l_, i_, grp):
        src, dst, nblk, rows = cast_specs[nm_]
        r0 = (l_ * nblk + i_) * rows
        k.dma("gpsimd", dst[r0:r0 + rows, :], src[r0:r0 + rows, :], writes=[U(nm_, (l_, i_))], grp=grp)

    def issue_casts(n):
        grp = []
        for _ in range(n):
            if cast_queue:
                emit_cast(*cast_queue.pop(0), grp=grp)

    for i_ in range(NIB):
        emit_cast("b_inb", 0, i_, None)

    ARENA_BYTES = 188 * 1024
    arena = nc.alloc_sbuf_tensor("arena", [128, ARENA_BYTES // 2], BF16)

    class Carver:
        def __init__(self):
            self.off = 0

        def take(self, shape, dt):
            esz = 4 if dt == F32 else 2
            n = int(np.prod(shape[1:]))
            self.off = (self.off + 63) // 64 * 64
            a = arena[0:shape[0], self.off // 2: self.off // 2 + n * esz // 2]
            if dt == F32:
                a = a.bitcast(F32)
            self.off += n * esz
            assert self.off <= ARENA_BYTES, (self.off, ARENA_BYTES)
            if len(shape) == 3:
                a = a.rearrange("p (a b) -> p a b", a=shape[1])
            elif len(shape) == 4:
                a = a.rearrange("p (a b c) -> p a b c", a=shape[1], b=shape[2])
            return a

    arena_u = Unit("arena")

    def wload(cv_bufs, cv_units, slot, src2d, rows, deps, shape3):
        a, b = shape3
        k.dma("sync", cv_bufs[slot][:, 0:a, 0:b],
              src2d.rearrange("(p r) f -> p (r f)", p=128).rearrange("p (a b) -> p a b", a=a),
              reads=deps, writes=[cv_units[slot]])

    def phase1(l):
        cv = Carver()
        xn = cv.take([128, NCH, P1_T], BF16)
        xn_u = k.units(P1_NS, "xn")
        hb = cv.take([128, NCH, P1_SW], F32)
        hb_u = Unit("hb")
        sq = cv.take([128, NCH, P1_SW], BF16)
        sq_u = Unit("sq")
        r1 = cv.take([128, P1_SW], F32)
        r1_u = Unit("r1")
        rstd = cv.take([128, P1_SW], F32)
        rstd_u = Unit("rstd")
        wb = [cv.take([128, NCH, 512], BF16) for _ in range(3)]
        wb_u = k.units(3, "wb")
        st2 = [cv.take([128, P1_T], BF16) for _ in range(4)]
        st2_u = k.units(4, "st2")
        st1 = [cv.take([128, 512], BF16) for _ in range(4)]
        st1_u = k.units(4, "st1")
        gaT = cv.take([33, P1_T], BF16)
        gaT_u = Unit("gaT")
        et = [cv.take([128, 512], F32) for _ in range(2)]
        et_u = k.units(2, "et")
        lst = [cv.take([128, 1024], F32) for _ in range(2)]
        lst_u = k.units(2, "lst")
        ft = cv.take([128, 8], F32)
        ft_u = Unit("ft")
        fe = cv.take([128, 8], F32)
        fe_u = Unit("fe")
        nst = cv.take([128, P1_SB, 8], F32)
        nst_u = Unit("nst")

        src = xT if l == 0 else hT
        srcv = src.rearrange("c p t -> p c t")
        k.op("gpsimd", lambda e: e.memset(gaT[0:33, :], 1.0), reads=[arena_u], writes=[gaT_u])
        k.op("gpsimd", lambda e: e.memset(gaT[0:32, :], 0.0), writes=[gaT_u])
        cnt = {"ps": 0, "st2": 0, "st1": 0, "w": 0, "ep": 0, "l": 0}

        def nextps():
            i = cnt["ps"] % 6
            cnt["ps"] += 1
            return PS[i], PSU[i]

        blocks = []
        for i in range(2):
            blocks.append(("ii", i, "copy", qT, i * 4, 128 ** -0.5))
        for i in range(2):
            blocks.append(("ii", 2 + i, "copy", kT, i * 4, 1.0))
        for i in range(2):
            blocks.append(("i", 4 + i, "copy", v_tok, i * 512, 1.0))
        for i in range(2):
            blocks.append(("ii", 6 + i, "copy", gqT, i * 4, 1.0 / 16))
        for i in range(2):
            blocks.append(("ii", 8 + i, "copy", gkT, i * 4, 1.0))
        for i in range(4):
            blocks.append(("i", 10 + i, "copy", gv_tok, i * 512, 1.0))
        for i in range(4):
            blocks.append(("ii", 14 + i, "silu", sgrT, i * 4, 1.0))
        for i in range(4):
            blocks.append(("ii", 18 + i, "sigm", sgfT, i * 4, 1.0))
        for i in range(4):
            blocks.append(("ii", 22 + i, "sigm", sggT, i * 4, 1.0))

        for st in range(NBLK // P1_SB):
            t0 = st * P1_T
            if l == 0:
                issue_casts(8)
            for s in range(P1_NS):
                c0 = t0 + s * P1_SW
                rd = [arena_u] if l == 0 else [arena_u] + [U("hT", j) for j in range(NBLK // P4_SB)]
                k.dma("sync", hb, srcv[:, :, c0:c0 + P1_SW], reads=rd, writes=[hb_u])
                k.op("scalar", lambda e: e.activation(out=sq, in_=hb, func=AF.Square), reads=[hb_u], writes=[sq_u])
                for c in range(NCH):
                    k.op("tensor", lambda e, c=c: e.matmul(PS[6][:, 0:P1_SW], lhsT=ones, rhs=sq[:, c, :], start=(c == 0), stop=(c == NCH - 1)),
                         reads=[sq_u] + CONST, writes=[PSU[6]])
                k.op("vector", lambda e: e.tensor_scalar(out=r1, in0=PS[6][:, 0:P1_SW], scalar1=1.0 / D, scalar2=EPS, op0=ALU.mult, op1=ALU.add),
                     reads=[PSU[6]], writes=[r1_u])
                k.op("vector", lambda e: e.reciprocal(out=r1, in_=r1), reads=[r1_u], writes=[r1_u])
                k.op("scalar", lambda e: e.activation(out=rstd, in_=r1, func=AF.Sqrt), reads=[r1_u], writes=[rstd_u])
                for c in range(NCH):
                    k.op("vector", lambda e, c=c, s=s: e.scalar_tensor_tensor(
                        out=xn[:, c, s * P1_SW:(s + 1) * P1_SW], in0=hb[:, c, :], scalar=gv_sb[:, 0, l, c:c + 1], in1=rstd,
                        op0=ALU.mult, op1=ALU.mult), reads=[hb_u, rstd_u] + CONST, writes=[xn_u[s]])

            def blk_units(b):
                lo = (b * 128) // P1_SW
                hi = (b * 128 + 127) // P1_SW
                return [xn_u[j] for j in range(lo, hi + 1)]

            wsm = wsm_sb[:, l, :, :]
            for s in range(P1_NS):
                ps, psu = nextps()
                for c in range(NCH):
                    k.op("tensor", lambda e, c=c, s=s, ps=ps: e.matmul(ps[0:16, 0:P1_SW], lhsT=wsm[:, c, 8:24], rhs=xn[:, c, s * P1_SW:(s + 1) * P1_SW],
                                                                    start=(c == 0), stop=(c == NCH - 1)), reads=[xn_u[s]] + CONST, writes=[psu])
                k.op("vector", lambda e, s=s, ps=ps: e.tensor_copy(out=gaT[0:16, s * P1_SW:(s + 1) * P1_SW], in_=ps[0:16, 0:P1_SW]),
                     reads=[psu], writes=[gaT_u])
            for b in range(P1_SB):
                gb = st * P1_SB + b
                ps, psu = nextps()
                for c in range(NCH):
                    k.op("tensor", lambda e, c=c, b=b, ps=ps: e.matmul(ps[:, 0:8], lhsT=xn[:, c, b * 128:(b + 1) * 128], rhs=wsm[:, c, 0:8],
                                                                    start=(c == 0), stop=(c == NCH - 1)), reads=blk_units(b) + CONST, writes=[psu])
                k.op("vector", lambda e, ps=ps: e.tensor_tensor(out=ft, in0=ps[:, 0:8], in1=bfor_sb[:, l, :], op=ALU.add), reads=[psu] + CONST, writes=[ft_u])
                k.op("scalar", lambda e: e.activation(out=fe, in_=ft, func=AF.Exp, scale=-1.0), reads=[ft_u], writes=[fe_u])
                k.op("scalar", lambda e, b=b: e.activation(out=nst[:, b, :], in_=fe, func=AF.Ln, bias=1.0), reads=[fe_u], writes=[nst_u])
                if gb == 0:
                    k.op("scalar", lambda e, b=b: e.activation(out=nst[0:PADT, b, :], in_=fe[0:PADT, :], func=AF.Copy, scale=0.0), reads=[fe_u], writes=[nst_u])
                li = cnt["l"] % 2
                cnt["l"] += 1
                for half in range(2):
                    ps2, psu2 = nextps()
                    k.op("tensor", lambda e, b=b, half=half, ps2=ps2: e.matmul(ps2[:, :], lhsT=gaT[0:33, b * 128:(b + 1) * 128],
                                                                              rhs=waug_sb[0:33, l, half * 512:(half + 1) * 512], start=True, stop=True),
                         reads=[gaT_u] + CONST, writes=[psu2])
                    k.op("scalar", lambda e, half=half, ps2=ps2: e.activation(out=et[half], in_=ps2[:, :], func=AF.Exp, scale=-1.0), reads=[psu2], writes=[et_u[half]])
                    k.op("scalar", lambda e, half=half, li=li: e.activation(out=lst[li][:, half * 512:(half + 1) * 512], in_=et[half], func=AF.Ln, bias=1.0),
                         reads=[et_u[half]], writes=[lst_u[li]])
                if gb == 0:
                    k.op("gpsimd", lambda e, li=li: e.memset(lst[li][0:PADT, :], 0.0), reads=[lst_u[li]], writes=[lst_u[li]])
                k.dma("sync", l_tok[gb * 128:(gb + 1) * 128, :], lst[li], reads=[lst_u[li]], writes=[U("l_tok", gb)])
            k.dma("sync", nlf.rearrange("(b p) h -> p b h", p=128)[:, st * P1_SB:(st + 1) * P1_SB, :], nst, reads=[nst_u], writes=[U("nlf", st)])

            def load(bi):
                kind, blk = blocks[bi][0], blocks[bi][1]
                slot = cnt["w"] % 3
                cnt["w"] += 1
                r0 = (l * NIB + blk) * 512
                wload(wb, wb_u, slot, b_inb[r0:r0 + 512, :], 512, [U("b_inb", (l, blk))], (NCH, 512))
                return slot
            slots = {}
            slots[0] = load(0)
            slots[1] = load(1)
            for bi, (kind, blk, ep, dst, d0, scale) in enumerate(blocks):
                if bi + 2 < len(blocks):
                    slots[bi + 2] = load(bi + 2)
                slot = slots[bi]
                w = wb[slot]
                wu = wb_u[slot]
                if kind == "ii":
                    for mc in range(4):
                        si = cnt["st2"] % 4
                        cnt["st2"] += 1
                        for s in range(P1_NS):
                            ps, psu = nextps()
                            for c in range(NCH):
                                k.op("tensor", lambda e, c=c, s=s, mc=mc, ps=ps, w=w: e.matmul(
                                    ps[:, 0:P1_SW], lhsT=w[:, c, mc * 128:(mc + 1) * 128], rhs=xn[:, c, s * P1_SW:(s + 1) * P1_SW],
                                    start=(c == 0), stop=(c == NCH - 1)), reads=[wu, xn_u[s]], writes=[psu])
                            o = st2[si][:, s * P1_SW:(s + 1) * P1_SW]
                            i_ = ps[:, 0:P1_SW]
                            if ep == "silu":
                                k.op("scalar", lambda e, o=o, i_=i_: e.activation(out=o, in_=i_, func=AF.Silu), reads=[psu], writes=[st2_u[si]])
                            elif ep == "sigm":
                                k.op("scalar", lambda e, o=o, i_=i_: e.activation(out=o, in_=i_, func=AF.Sigmoid), reads=[psu], writes=[st2_u[si]])
                            else:
                                cnt["ep"] += 1
                                if si % 2 == 0:
                                    k.op("scalar", lambda e, o=o, i_=i_, scale=scale: e.activation(out=o, in_=i_, func=AF.Copy, scale=scale), reads=[psu], writes=[st2_u[si]])
                                else:
                                    k.op("vector", lambda e, o=o, i_=i_, scale=scale: e.tensor_scalar_mul(out=o, in0=i_, scalar1=scale), reads=[psu], writes=[st2_u[si]])
                        k.dma("sync", dst[d0 + mc, :, t0:t0 + P1_T], st2[si], reads=[st2_u[si]], writes=[U(dst.name if hasattr(dst, "name") else "d", (d0 + mc, st))])
                else:
                    for b in range(P1_SB):
                        gb = st * P1_SB + b
                        si = cnt["st1"] % 4
                        cnt["st1"] += 1
                        ps, psu = nextps()
                        for c in range(NCH):
                            k.op("tensor", lambda e, c=c, b=b, ps=ps, w=w: e.matmul(ps[:, :], lhsT=xn[:, c, b * 128:(b + 1) * 128], rhs=w[:, c, :],
                                                                                    start=(c == 0), stop=(c == NCH - 1)), reads=[wu] + blk_units(b), writes=[psu])
                        if si % 2 == 0:
                            k.op("scalar", lambda e, si=si, ps=ps: e.activation(out=st1[si], in_=ps[:, :], func=AF.Copy), reads=[psu], writes=[st1_u[si]])
                        else:
                            k.op("vector", lambda e, si=si, ps=ps: e.tensor_copy(out=st1[si], in_=ps[:, :]), reads=[psu], writes=[st1_u[si]])
                        k.dma("sync", dst[gb * 128:(gb + 1) * 128, d0:d0 + 512], st1[si], reads=[st1_u[si]], writes=[U(dst.name if hasattr(dst, "name") else "d", (gb, d0))])
        allu = xn_u + [hb_u, sq_u, r1_u, rstd_u, gaT_u, ft_u, fe_u, nst_u] + wb_u + st2_u + st1_u + et_u + lst_u
        k.op("gpsimd", lambda e: e.memset(arena[0:1, 0:2], 0.0), writes=allu + PSU + [arena_u])

    def names(prefix):
        return [u for (n, kk), u in du.items() if n == prefix]

    def phase2(l):
        cv = Carver()
        nl_sb = cv.take([128, NBLK, 8], F32)
        nl_u = Unit("nl")
        cl = cv.take([128, NBLK, 8], F32)
        cl_u = Unit("cl")
        totA = cv.take([128, NBLK, 8], F32)
        totB = cv.take([128, NBLK, 8], F32)
        tot_u = [Unit("totA"), Unit("totB")]
        refb = cv.take([128, NBLK, 8], F32)
        ref_u = Unit("refb")
        cK = cv.take([128, NBLK, 8], F32)
        cK_u = Unit("cK")
        cR = cv.take([128, NBLK, 8], F32)
        cR_u = Unit("cR")
        qs = [cv.take([128, TG], BF16) for _ in range(2)]
        ks = [cv.take([128, TG], BF16) for _ in range(2)]
        vs = [cv.take([128, NBLK, 128], BF16) for _ in range(2)]
        ld_u = [k.units(2, "q"), k.units(2, "k"), k.units(2, "v")]
        ofs = [cv.take([128, TG], BF16) for _ in range(2)]
        ofs_u = k.units(2, "ofs")
        Bj = [cv.take([128, NBLK], F32) for _ in range(2)]
        Bj_u = k.units(2, "Bj")
        NP = 8
        pb = [cv.take([128, 128], BF16) for _ in range(NP)]
        pb_u = k.units(NP, "pb")
        dn = cv.take([128, 128], F32)
        dn_u = Unit("dn")
        s_u = k.units(8, "sslot")

        k.dma("sync", nl_sb, nlf.rearrange("(b p) h -> p b h", p=128), reads=[arena_u] + names("nlf"), writes=[nl_u])
        nlflat = nl_sb.rearrange("p b h -> p (b h)")
        NF = NBLK * 8
        k.op("tensor", lambda e: e.matmul(PS[6][:, 0:NF], lhsT=Uneg, rhs=nlflat, start=True, stop=True), reads=[nl_u] + CONST, writes=[PSU[6]])
        k.op("vector", lambda e: e.tensor_copy(out=cl.rearrange("p b h -> p (b h)"), in_=PS[6][:, 0:NF]), reads=[PSU[6]], writes=[cl_u])
        clflat = cl.rearrange("p b h -> p (b h)")
        k.op("tensor", lambda e: e.matmul(PS[7][:, 0:NF], lhsT=E127, rhs=clflat, start=True, stop=True), reads=[cl_u] + CONST, writes=[PSU[7]])
        k.op("tensor", lambda e: e.matmul(PS[6][:, 0:NF], lhsT=E63, rhs=clflat, start=True, stop=True), reads=[cl_u] + CONST, writes=[PSU[6]])
        k.op("vector", lambda e: e.tensor_copy(out=totA.rearrange("p b h -> p (b h)"), in_=PS[7][:, 0:NF]), reads=[PSU[7]], writes=[tot_u[0]])
        k.op("vector", lambda e: e.tensor_copy(out=refb.rearrange("p b h -> p (b h)"), in_=PS[6][:, 0:NF]), reads=[PSU[6]], writes=[ref_u])
        cur, oth = (totA, 0), (totB, 1)
        d = 1
        while d < NBLK:
            (ca, ci), (oa, oi) = cur, oth
            k.op("vector", lambda e, ca=ca, oa=oa, d=d: e.tensor_copy(out=oa[:, 0:d, :], in_=ca[:, 0:d, :]), reads=[tot_u[ci]], writes=[tot_u[oi]])
            k.op("vector", lambda e, ca=ca, oa=oa, d=d: e.tensor_tensor(out=oa[:, d:NBLK, :], in0=ca[:, d:NBLK, :], in1=ca[:, 0:NBLK - d, :], op=ALU.add),
                 reads=[tot_u[ci]], writes=[tot_u[oi]])
            cur, oth = oth, cur
            d *= 2
        inc, inc_i = cur
        k.op("vector", lambda e: e.tensor_copy(out=cK[:, 0:1, :], in_=cl[:, 0:1, :]), reads=[cl_u], writes=[cK_u])
        k.op("vector", lambda e: e.tensor_tensor(out=cK[:, 1:NBLK, :], in0=cl[:, 1:NBLK, :], in1=inc[:, 0:NBLK - 1, :], op=ALU.add),
             reads=[cl_u, tot_u[inc_i]], writes=[cK_u])
        k.op("vector", lambda e: e.tensor_scalar(out=cK[:, 0, :], in0=cK[:, 0, :], scalar1=padv_sb[:, 0:1], scalar2=None, op0=ALU.add),
             reads=[cK_u] + CONST, writes=[cK_u])
        k.op("vector", lambda e: e.tensor_copy(out=cR[:, 0:1, :], in_=refb[:, 0:1, :]), reads=[ref_u], writes=[cR_u])
        k.op("vector", lambda e: e.tensor_tensor(out=cR[:, 1:NBLK, :], in0=refb[:, 1:NBLK, :], in1=inc[:, 0:NBLK - 1, :], op=ALU.add),
             reads=[ref_u, tot_u[inc_i]], writes=[cR_u])

        def loadhead(h):
            s = h % 2
            k.dma("sync", qs[s], qT[h], reads=names("qT"), writes=[ld_u[0][s]])
            k.dma("sync", ks[s], kT[h], reads=names("kT"), writes=[ld_u[1][s]])
            k.dma("sync", vs[s], v_tok.rearrange("(b p) f -> p b f", p=128)[:, :, h * 128:(h + 1) * 128], reads=names("v_tok"), writes=[ld_u[2][s]])

        LA = 3
        loadhead(0)
        gcount = [0]
        for h in range(8):
            if l == 0:
                issue_casts(6)
            if h + 1 < 8:
                loadhead(h + 1)
            hs = h % 2
            q_, k_, v_ = qs[hs], ks[hs], vs[hs]
            qu, ku, vu = ld_u[0][hs], ld_u[1][hs], ld_u[2][hs]
            pairs = [(j, i) for j in range(NBLK) for i in range(j + 1)]
            info = {}
            for idx in range(len(pairs) + LA):
                if idx < len(pairs):
                    j, i = pairs[idx]
                    if i == 0:
                        bj, bju = Bj[j % 2], Bj_u[j % 2]
                        k.op("vector", lambda e, j=j, bj=bj, h=h: e.tensor_scalar(
                            out=bj[:, 0:j + 1], in0=cK[:, 0:j + 1, h], scalar1=-1.0, scalar2=cR[:, j, h:h + 1], op0=ALU.mult, op1=ALU.add),
                            reads=[cK_u, cR_u], writes=[bju])
                    g = gcount[0]
                    gcount[0] += 1
                    sl = (0, 1, 6, 7)[g % 4]
                    sps = PS[sl][:, 0:128]
                    k.op("tensor", lambda e, sps=sps, i=i, j=j, k_=k_, q_=q_: e.matmul(sps, lhsT=k_[:, i * 128:(i + 1) * 128], rhs=q_[:, j * 128:(j + 1) * 128], start=True, stop=True),
                         reads=[ku, qu], writes=[PSU[sl]])
                    p = pb[g % NP]
                    pu = pb_u[g % NP]
                    bj, bju = Bj[j % 2], Bj_u[j % 2]
                    k.op("scalar", lambda e, p=p, sps=sps, bj=bj, i=i: e.activation(out=p, in_=sps, func=AF.Exp, bias=bj[:, i:i + 1], scale=1.0),
                         reads=[PSU[sl], bju], writes=[pu])
                    if i == j:
                        k.op("gpsimd", lambda e, p=p: e.tensor_tensor(out=p, in0=p, in1=trimask, op=ALU.mult), reads=[pu] + CONST, writes=[pu])
                    info[idx] = (p, pu)
                if idx - LA >= 0:
                    j, i = pairs[idx - LA]
                    p, pu = info.pop(idx - LA)
                    ob = 2 + (j % 2)
                    db = 4 + (j % 2)
                    k.op("tensor", lambda e, p=p, i=i, j=j, ob=ob, v_=v_: e.matmul(PS[ob][:, 0:128], lhsT=v_[:, i, :], rhs=p, start=(i == 0), stop=(i == j)),
                         reads=[pu, vu], writes=[PSU[ob]])
                    k.op("tensor", lambda e, p=p, i=i, j=j, db=db: e.matmul(PS[db][:, 0:128], lhsT=ones, rhs=p, start=(i == 0), stop=(i == j)),
                         reads=[pu] + CONST, writes=[PSU[db]])
                    if i == j:
                        k.op("vector", lambda e, db=db: e.tensor_scalar_add(out=dn, in0=PS[db][:, 0:128], scalar1=1e-30), reads=[PSU[db]], writes=[dn_u])
                        k.op("vector", lambda e: e.reciprocal(out=dn, in_=dn), reads=[dn_u], writes=[dn_u])
                        k.op("vector", lambda e, ob=ob, j=j, hs=hs: e.tensor_tensor(out=ofs[hs][:, j * 128:(j + 1) * 128], in0=PS[ob][:, 0:128], in1=dn, op=ALU.mult),
                             reads=[PSU[ob], dn_u], writes=[ofs_u[hs]])
            k.dma("sync", ofoxT[h], ofs[hs], reads=[ofs_u[hs]], writes=[U("ofoxT", h)])
        allu = s_u + [nl_u, cl_u, ref_u, cK_u, cR_u, dn_u] + tot_u + ld_u[0] + ld_u[1] + ld_u[2] + ofs_u + Bj_u + pb_u
        k.op("gpsimd", lambda e: e.memset(arena[0:1, 0:2], 0.0), writes=allu + PSU + [arena_u])

    def phase3(l):
        cv = Carver()
        gq = cv.take([128, 2, TG], BF16)
        gk = cv.take([128, 2, TG], BF16)
        gvv = cv.take([128, NBLK, 512], BF16)
        lt = cv.take([128, NBLK, 256], F32)
        in_u = k.units(4, "p3in")
        ons = cv.take([128, 4, TG], BF16)
        ons_u = Unit("ons")
        NA = 3
        eb2 = [cv.take([128, 2, 128], F32) for _ in range(NA)]
        eb2_u = k.units(NA, "eb")
        enb2 = [cv.take([128, 2, 128], F32) for _ in range(NA)]
        enb2_u = k.units(NA, "enb")
        qd2 = [cv.take([128, 2, 128], BF16) for _ in range(NA)]
        qd2_u = k.units(NA, "qd")
        ki2 = [cv.take([128, 2, 128], BF16) for _ in range(NA)]
        ki2_u = k.units(NA, "ki")
        kit2 = [cv.take([128, 256], BF16) for _ in range(NA)]
        kit2_u = k.units(NA, "kit")
        am2 = [cv.take([128, 128], BF16) for _ in range(NA)]
        am2_u = k.units(NA, "am")
        S32 = cv.take([128, 2, 512], F32)
        S32_u = Unit("S32")
        Sbf = cv.take([128, 2, 512], BF16)
        Sbf_u = Unit("Sbf")
        dlp = cv.take([128, 2], F32)
        dlp_u = Unit("dlp")
        sqo = cv.take([128, 4, 128], BF16)
        sqo_u = Unit("sqo")
        rr = cv.take([128, 128], F32)
        rr_u = Unit("rr")
        rs = cv.take([128, 128], F32)
        rs_u = Unit("rs")
        bT = PS[0][:, 0:256].rearrange("p (a b) -> p a b", a=2)
        trp = PS[1][:, 0:128].bitcast(BF16)
        Aps = PS[2][:, 0:128]
        ops_ = PS[3][:, :].rearrange("p (a b) -> p a b", a=4)
        Xps = [PS[4], PS[5]]
        ssp = PS[6][:, 0:128]
        first = True
        for hd in range(4):
            if l == 0:
                issue_casts(6)
            rd0 = [arena_u] if first else []
            first = False
            k.dma("sync", gq, gqT[2 * hd:2 * hd + 2].rearrange("c p t -> p c t"), reads=rd0 + names("gqT"), writes=[in_u[0]])
            k.dma("sync", gk, gkT[2 * hd:2 * hd + 2].rearrange("c p t -> p c t"), reads=rd0 + names("gkT"), writes=[in_u[1]])
            k.dma("sync", gvv, gv_tok.rearrange("(b p) f -> p b f", p=128)[:, :, hd * 512:(hd + 1) * 512], reads=rd0 + names("gv_tok"), writes=[in_u[2]])
            k.dma("sync", lt, l_tok.rearrange("(b p) f -> p b f", p=128)[:, :, hd * 256:(hd + 1) * 256], reads=rd0 + names("l_tok"), writes=[in_u[3]])
            def stageA(n):
                eb, eb_u, enb, enb_u = eb2[n % NA], eb2_u[n % NA], enb2[n % NA], enb2_u[n % NA]
                qd, qd_u, ki, ki_u = qd2[n % NA], qd2_u[n % NA], ki2[n % NA], ki2_u[n % NA]
                kit, kit_u, am, am_u = kit2[n % NA], kit2_u[n % NA], am2[n % NA], am2_u[n % NA]
                tsl = slice(n * 128, (n + 1) * 128)
                for kc in range(2):
                    k.op("tensor", lambda e, kc=kc, n=n: e.matmul(bT[:, kc, :], lhsT=lt[:, n, kc * 128:(kc + 1) * 128], rhs=Uneg16, start=True, stop=True),
                         reads=[in_u[3]] + CONST, writes=[PSU[0]])
                k.op("scalar", lambda e: e.activation(out=eb, in_=bT, func=AF.Exp), reads=[PSU[0]], writes=[eb_u])
                k.op("scalar", lambda e: e.activation(out=enb, in_=bT, func=AF.Exp, scale=-1.0), reads=[PSU[0]], writes=[enb_u])
                k.op("vector", lambda e, tsl=tsl: e.tensor_tensor(out=qd, in0=gq[:, :, tsl], in1=eb, op=ALU.mult), reads=[in_u[0], eb_u], writes=[qd_u])
                k.op("gpsimd", lambda e, tsl=tsl: e.tensor_tensor(out=ki, in0=gk[:, :, tsl], in1=enb, op=ALU.mult), reads=[in_u[1], enb_u], writes=[ki_u])
                for kc in range(2):
                    k.op("tensor", lambda e, kc=kc: e.transpose(trp[:, kc * 128:(kc + 1) * 128], ki[:, kc, :], ident), reads=[ki_u] + CONST, writes=[PSU[1]])
                k.op("scalar", lambda e: e.activation(out=kit, in_=trp, func=AF.Copy), reads=[PSU[1]], writes=[kit_u])
                for kc in range(2):
                    k.op("tensor", lambda e, kc=kc: e.matmul(Aps, lhsT=ki[:, kc, :], rhs=qd[:, kc, :], start=(kc == 0), stop=(kc == 1)),
                         reads=[ki_u, qd_u], writes=[PSU[2]])
                k.op("vector", lambda e: e.tensor_tensor(out=am, in0=Aps, in1=trimask, op=ALU.mult), reads=[PSU[2]] + CONST, writes=[am_u])
            def stageB(n):
                eb, eb_u = eb2[n % NA], eb2_u[n % NA]
                qd, qd_u = qd2[n % NA], qd2_u[n % NA]
                kit, kit_u, am, am_u = kit2[n % NA], kit2_u[n % NA], am2[n % NA], am2_u[n % NA]
                ob = 3 if n % 2 == 0 else 7
                ops_ = PS[ob][:, :].rearrange("p (a b) -> p a b", a=4)
                for dvc in range(4):
                    k.op("tensor", lambda e, dvc=dvc, n=n: e.matmul(ops_[:, dvc, :], lhsT=gvv[:, n, dvc * 128:(dvc + 1) * 128], rhs=am, start=True, stop=(n == 0)),
                         reads=[in_u[2], am_u], writes=[PSU[ob]])
                    if n > 0:
                        for kc in range(2):
                            k.op("tensor", lambda e, dvc=dvc, kc=kc: e.matmul(ops_[:, dvc, :], lhsT=Sbf[:, kc, dvc * 128:(dvc + 1) * 128], rhs=qd[:, kc, :],
                                                                              start=False, stop=(kc == 1)), reads=[Sbf_u, qd_u], writes=[PSU[ob]])
                for kc in range(2):
                    k.op("tensor", lambda e, kc=kc, n=n: e.matmul(Xps[kc][:, :], lhsT=kit[:, kc * 128:(kc + 1) * 128], rhs=gvv[:, n, :], start=True, stop=True),
                         reads=[kit_u, in_u[2]], writes=[PSU[4 + kc]])
                    if n == 0:
                        k.op("vector", lambda e, kc=kc: e.tensor_copy(out=S32[:, kc, :], in_=Xps[kc][:, :]), reads=[PSU[4 + kc]], writes=[S32_u])
                    else:
                        k.op("vector", lambda e, kc=kc: e.scalar_tensor_tensor(out=S32[:, kc, :], in0=S32[:, kc, :], scalar=dlp[:, kc:kc + 1], in1=Xps[kc][:, :],
                                                                              op0=ALU.mult, op1=ALU.add), reads=[PSU[4 + kc], S32_u, dlp_u], writes=[S32_u])
                k.op("vector", lambda e: e.tensor_copy(out=dlp, in_=eb[:, :, 127]), reads=[eb_u, S32_u], writes=[dlp_u])
                for kc in range(2):
                    k.op("vector", lambda e, kc=kc: e.tensor_scalar_mul(out=Sbf[:, kc, :], in0=S32[:, kc, :], scalar1=dlp[:, kc:kc + 1]),
                         reads=[S32_u, dlp_u], writes=[Sbf_u])
            def stageC(n):
                tsl = slice(n * 128, (n + 1) * 128)
                ob = 3 if n % 2 == 0 else 7
                ops_ = PS[ob][:, :].rearrange("p (a b) -> p a b", a=4)
                k.op("scalar", lambda e: e.activation(out=sqo, in_=ops_, func=AF.Square), reads=[PSU[ob]], writes=[sqo_u])
                for dvc in range(4):
                    k.op("tensor", lambda e, dvc=dvc: e.matmul(ssp, lhsT=ones, rhs=sqo[:, dvc, :], start=(dvc == 0), stop=(dvc == 3)), reads=[sqo_u] + CONST, writes=[PSU[6]])
                k.op("scalar", lambda e: e.activation(out=rr, in_=ssp, func=AF.Ln, scale=1.0 / 512, bias=eps_sb[:, 0:1]), reads=[PSU[6]] + CONST, writes=[rr_u])
                k.op("scalar", lambda e: e.activation(out=rs, in_=rr, func=AF.Exp, scale=-0.5), reads=[rr_u], writes=[rs_u])
                for dvc in range(4):
                    k.op("vector", lambda e, dvc=dvc, tsl=tsl, hd=hd: e.scalar_tensor_tensor(
                        out=ons[:, dvc, tsl], in0=ops_[:, dvc, :], scalar=gv_sb[:, 2, l, hd * 4 + dvc:hd * 4 + dvc + 1], in1=rs, op0=ALU.mult, op1=ALU.mult),
                        reads=[PSU[ob], rs_u] + CONST, writes=[ons_u])
            stageA(0)
            stageA(1)
            for n in range(NBLK):
                stageB(n)
                if n >= 1:
                    stageC(n - 1)
                if n + 2 < NBLK:
                    stageA(n + 2)
            stageC(NBLK - 1)
            k.dma("sync", onT[4 * hd:4 * hd + 4].rearrange("c p t -> p c t"), ons, reads=[ons_u], writes=[U("onT", hd)])
        allu = in_u + [ons_u, S32_u, Sbf_u, dlp_u, sqo_u, rr_u, rs_u] + eb2_u + enb2_u + qd2_u + ki2_u + kit2_u + am2_u
        k.op("gpsimd", lambda e: e.memset(arena[0:1, 0:2], 0.0), writes=allu + PSU + [arena_u])

    def phase4(l, last):
        T = P4_T
        cv = Carver()
        h = cv.take([128, NCH, T], F32)
        h_u = k.units(NCH, "h")
        of = cv.take([128, 8, T], BF16)
        of_u = Unit("of")
        A = cv.take([128, NCH, T], BF16)
        A_u = Unit("A")
        sr = cv.take([128, NCH, T], BF16)
        sr_u = Unit("sr")
        y = cv.take([128, NCH, T], BF16)
        y_u = k.units(NCH, "y")
        a = cv.take([128, 64, T], BF16)
        a_u = k.units(64, "a")
        gts = [cv.take([128, 2, T], BF16) for _ in range(2)]
        gts_u = k.units(2, "gts")
        gtg_u = k.units(2, "gtg")
        t1 = [cv.take([128, T], F32) for _ in range(2)]
        t1_u = k.units(2, "t1")
        t2 = [cv.take([128, T], F32) for _ in range(2)]
        t2_u = k.units(2, "t2")
        rl = [cv.take([128, T], BF16) for _ in range(2)]
        rl_u = k.units(2, "rl")
        r1 = cv.take([128, T], F32)
        r1_u = Unit("r1")
        rstd = cv.take([128, T], F32)
        rstd_u = Unit("rstd")
        wb = [cv.take([128, NCH, 512], BF16) for _ in range(3)]
        wb_u = k.units(3, "wb4")
        ost = a[:, 16:48, :].rearrange("p a b -> p (a b)").bitcast(F32).rearrange("p (c t) -> p c t", c=NCH) if last else None
        ost_u = Unit("ost")
        cnt = {"ps": 0, "w": 0, "g": 0, "t": 0}

        def nextps():
            i = cnt["ps"] % 6
            cnt["ps"] += 1
            return PS[i], PSU[i]

        stream = []
        for g in range(8):
            stream.append(("yc", g))
        for g in range(4):
            stream.append(("ot", g))
        for g in range(16):
            stream.append(("f1", g))
        for g in range(16):
            stream.append(("f2", g))

        def load(si):
            nm, g = stream[si]
            slot = cnt["w"] % 3
            cnt["w"] += 1
            if nm == "yc":
                r0 = (l * 8 + g) * 384
                k.dma("sync", wb[slot].rearrange("p a b -> p (a b)")[:, 0:6144], b_yc[r0:r0 + 384, :].rearrange("(p r) f -> p (r f)", p=128),
                      reads=[U("b_yc", (l, g))], writes=[wb_u[slot]])
            elif nm == "ot":
                r0 = (l * 4 + g) * 512
                wload(wb, wb_u, slot, b_ot[r0:r0 + 512, :], 512, [U("b_ot", (l, g))], (NCH, 512))
            elif nm == "f1":
                r0 = (l * 16 + g) * 512
                wload(wb, wb_u, slot, b_f1[r0:r0 + 512, :], 512, [U("b_f1", (l, g))], (NCH, 512))
            else:
                r0 = (l * 16 + g) * 512
                k.dma("sync", wb[slot].rearrange("p a b -> p (a b)"), b_f2[r0:r0 + 512, :].rearrange("(p r) f -> p (r f)", p=128),
                      reads=[U("b_f2", (l, g))], writes=[wb_u[slot]])
            return slot

        src = xT if l == 0 else hT
        srcv = src.rearrange("c p t -> p c t")
        first = True
        for st in range(NBLK // P4_SB):
            t0 = st * T
            tsl = slice(t0, t0 + T)
            issue_casts(3)
            rd0 = [arena_u] if first else []
            first = False
            rdh = rd0 + ([] if l == 0 else names("hT"))
            k.dma("sync", h, srcv[:, :, tsl], reads=rdh, writes=h_u)
            k.dma("sync", of, ofoxT.rearrange("c p t -> p c t")[:, :, tsl], reads=rd0 + names("ofoxT"), writes=[of_u])
            k.dma("sync", A, onT.rearrange("c p t -> p c t")[:, :, tsl], reads=rd0 + names("onT"), writes=[A_u])
            k.dma("sync", sr, sgrT.rearrange("c p t -> p c t")[:, :, tsl], reads=rd0 + names("sgrT"), writes=[sr_u])
            k.op("gpsimd", lambda e: e.tensor_tensor(out=A, in0=A, in1=sr, op=ALU.mult), reads=[A_u, sr_u], writes=[A_u])
            slots = {}
            slots[0] = load(0)
            slots[1] = load(1)
            si = 0

            def nxt():
                nonlocal si
                if si + 2 < len(stream):
                    slots[si + 2] = load(si + 2)
                s_ = slots[si]
                si += 1
                return wb[s_], wb_u[s_]

            for g in range(8):
                wyc, wfu = nxt()
                wgu = wfu
                wv_ = wyc.rearrange("p a b -> p (a b)")[:, 0:6144].rearrange("p (a b) -> p a b", a=24)
                wf = wv_[:, 0:8, :]
                wg = wv_[:, 8:24, :]
                for m in range(2):
                    oc = g * 2 + m
                    gi = cnt["g"] % 2
                    cnt["g"] += 1
                    k.dma("sync", gts[gi][:, 0, :], sgfT[oc][:, tsl], reads=names("sgfT"), writes=[gts_u[gi]])
                    k.dma("sync", gts[gi][:, 1, :], sggT[oc][:, tsl], reads=names("sggT"), writes=[gtg_u[gi]])
                    pf, pfu = nextps()
                    for kc in range(8):
                        k.op("tensor", lambda e, kc=kc, m=m, pf=pf, wf=wf: e.matmul(pf[:, 0:T], lhsT=wf[:, kc, m * 128:(m + 1) * 128], rhs=of[:, kc, :],
                                                                                start=(kc == 0), stop=(kc == 7)), reads=[wfu, of_u], writes=[pfu])
                    pg, pgu = nextps()
                    for kc in range(NCH):
                        k.op("tensor", lambda e, kc=kc, m=m, pg=pg, wg=wg: e.matmul(pg[:, 0:T], lhsT=wg[:, kc, m * 128:(m + 1) * 128], rhs=A[:, kc, :],
                                                                                start=(kc == 0), stop=(kc == NCH - 1)), reads=[wgu, A_u], writes=[pgu])
                    ti = cnt["t"] % 2
                    cnt["t"] += 1
                    k.op("vector", lambda e, ti=ti, gi=gi, pf=pf: e.tensor_tensor(out=t1[ti], in0=pf[:, 0:T], in1=gts[gi][:, 0, :], op=ALU.mult),
                         reads=[pfu, gts_u[gi]], writes=[t1_u[ti]])
                    k.op("vector", lambda e, ti=ti, gi=gi, pg=pg: e.tensor_tensor(out=t2[ti], in0=pg[:, 0:T], in1=gts[gi][:, 1, :], op=ALU.mult),
                         reads=[pgu, gtg_u[gi]], writes=[t2_u[ti]])
                    k.op("gpsimd", lambda e, ti=ti, oc=oc: e.tensor_tensor(out=y[:, oc, :], in0=t1[ti], in1=t2[ti], op=ALU.add),
                         reads=[t1_u[ti], t2_u[ti]], writes=[y_u[oc]])
            for g in range(4):
                w, wu = nxt()
                for m in range(4):
                    oc = g * 4 + m
                    ps, psu = nextps()
                    for kc in range(NCH):
                        k.op("tensor", lambda e, kc=kc, m=m, ps=ps, w=w: e.matmul(ps[:, 0:T], lhsT=w[:, kc, m * 128:(m + 1) * 128], rhs=y[:, kc, :],
                                                                              start=(kc == 0), stop=(kc == NCH - 1)), reads=[wu, y_u[kc]], writes=[psu])
                    k.op("vector", lambda e, oc=oc, ps=ps: e.tensor_tensor(out=h[:, oc, :], in0=h[:, oc, :], in1=ps[:, 0:T], op=ALU.add),
                         reads=[psu, h_u[oc]], writes=[h_u[oc]])

            if dbg_y is not None:
                k.dma("sync", dbg_y.rearrange("c p t -> p c t")[:, :, tsl], y, reads=y_u, writes=[U("dbg_y", st)])
            if dbg_h1 is not None:
                k.dma("sync", dbg_h1.rearrange("c p t -> p c t")[:, :, tsl], h, reads=h_u, writes=[U("dbg_h1", st)])
            def rms(dst_fn, gsel, dst_units):
                for c in range(NCH):
                    k.op("scalar", lambda e, c=c: e.activation(out=a[:, c, :], in_=h[:, c, :], func=AF.Square), reads=[h_u[c]], writes=[a_u[c]])
                for c in range(NCH):
                    k.op("tensor", lambda e, c=c: e.matmul(PS[6][:, 0:T], lhsT=ones, rhs=a[:, c, :], start=(c == 0), stop=(c == NCH - 1)),
                         reads=[a_u[c]] + CONST, writes=[PSU[6]])
                k.op("vector", lambda e: e.tensor_scalar(out=r1, in0=PS[6][:, 0:T], scalar1=1.0 / D, scalar2=EPS, op0=ALU.mult, op1=ALU.add), reads=[PSU[6]], writes=[r1_u])
                k.op("vector", lambda e: e.reciprocal(out=r1, in_=r1), reads=[r1_u], writes=[r1_u])
                k.op("scalar", lambda e: e.activation(out=rstd, in_=r1, func=AF.Sqrt), reads=[r1_u], writes=[rstd_u])
                for c in range(NCH):
                    k.op("vector", lambda e, c=c: e.scalar_tensor_tensor(out=dst_fn(c), in0=h[:, c, :], scalar=gv_sb[:, gsel, l if gsel != 3 else 0, c:c + 1], in1=rstd,
                                                                        op0=ALU.mult, op1=ALU.mult), reads=[h_u[c], rstd_u] + CONST, writes=dst_units(c))
            rms(lambda c: A[:, c, :], 1, lambda c: [A_u])
            for g in range(16):
                w, wu = nxt()
                for m in range(4):
                    fc = g * 4 + m
                    ps, psu = nextps()
                    for kc in range(NCH):
                        k.op("tensor", lambda e, kc=kc, m=m, ps=ps, w=w: e.matmul(ps[:, 0:T], lhsT=w[:, kc, m * 128:(m + 1) * 128], rhs=A[:, kc, :],
                                                                              start=(kc == 0), stop=(kc == NCH - 1)), reads=[wu, A_u], writes=[psu])
                    ri = fc % 2
                    k.op("scalar", lambda e, ri=ri, ps=ps: e.activation(out=rl[ri], in_=ps[:, 0:T], func=AF.Relu), reads=[psu], writes=[rl_u[ri]])
                    k.op("gpsimd", lambda e, ri=ri, fc=fc: e.tensor_tensor(out=a[:, fc, :], in0=rl[ri], in1=rl[ri], op=ALU.mult), reads=[rl_u[ri]], writes=[a_u[fc]])
            if dbg_a is not None:
                k.dma("sync", dbg_a.rearrange("c p t -> p c t")[:, :, tsl], a, reads=a_u, writes=[U("dbg_a", st)])
            for oc in range(16):
                w, wu = nxt()
                wv = w.rearrange("p a b -> p (a b)").rearrange("p (f c) -> p f c", f=64)
                ps, psu = nextps()
                for fc in range(64):
                    k.op("tensor", lambda e, fc=fc, ps=ps, wv=wv: e.matmul(ps[:, 0:T], lhsT=wv[:, fc, :], rhs=a[:, fc, :], start=(fc == 0), stop=(fc == 63)),
                         reads=[wu, a_u[fc]], writes=[psu])
                k.op("vector", lambda e, oc=oc, ps=ps: e.tensor_tensor(out=h[:, oc, :], in0=h[:, oc, :], in1=ps[:, 0:T], op=ALU.add),
                     reads=[psu, h_u[oc]], writes=[h_u[oc]])
            if last:
                rms(lambda c: ost[:, c, :], 3, lambda c: [ost_u] + a_u[16:48])
                fin = k.dma("sync", outT.rearrange("c p t -> p c t")[:, :, tsl], ost, reads=[ost_u] + a_u[16:48], writes=[U("outT", st)])
                finals.append(fin)
            else:
                k.dma("sync", hT.rearrange("c p t -> p c t")[:, :, tsl], h, reads=h_u, writes=[U("hT", st)])
        allu = h_u + [of_u, A_u, sr_u, r1_u, rstd_u, ost_u] + y_u + a_u + gts_u + gtg_u + t1_u + t2_u + rl_u + wb_u
        k.op("gpsimd", lambda e: e.memset(arena[0:1, 0:2], 0.0), writes=allu + PSU + [arena_u])

    finals = []
    done = False
    for l in range(nlayers):
        for ph, fn in ((1, phase1), (2, phase2), (3, phase3)):
            fn(l)
            if stop_after == (l, ph):
                done = True
                break
        if done:
            break
        phase4(l, l == nlayers - 1)
        if stop_after == (l, 4):
            break
    fw = list(finals)
    for (n, kk), u in du.items():
        if u.w is not None and u.w.dma and not n.startswith("b_"):
            fw.append(u.w)
    k.emit(final_wait_ops=fw)
    return nc, k


def _blockify(W, colblk):
    K_, F_ = W.shape
    return np.ascontiguousarray(W.reshape(K_ // 128, 128, F_ // colblk, colblk).transpose(2, 1, 0, 3))


def prep_shared(inp):
    sh = {}
    w_in = np.asarray(inp["w_in"], np.float32)
    offs = dict(fq=0, fk=1024, fv=2048, ff=3072, gq=3080, gk=4104, gv=5128, gr=7176, ga=9224, gf=9240, gg=11288)
    order = [("fq", 1024), ("fk", 1024), ("fv", 1024), ("gq", 1024), ("gk", 1024), ("gv", 2048), ("gr", 2048), ("gf", 2048), ("gg", 2048)]
    inb = []
    sm = []
    for l in range(NL):
        blks = []
        for nm, wd in order:
            blks.append(_blockify(w_in[l][:, offs[nm]:offs[nm] + wd], 512))
        inb.append(np.concatenate(blks, axis=0))
        small = np.concatenate([w_in[l][:, offs["ff"]:offs["ff"] + 8], w_in[l][:, offs["ga"]:offs["ga"] + 16]], axis=1)
        sm.append(small.reshape(NCH, 128, 24).transpose(1, 0, 2))
    sh["w_inb"] = np.ascontiguousarray(np.stack(inb)).reshape(-1, 2048)
    sh["w_sm"] = np.ascontiguousarray(np.stack(sm, axis=1))
    sh["w_yc"] = np.ascontiguousarray(np.stack([np.concatenate([_blockify(np.asarray(inp["w_o_fox"][l], np.float32), 256),
                                                                _blockify(np.asarray(inp["w_o_gla"][l], np.float32), 256)], axis=2)
                                                for l in range(NL)])).reshape(-1, 2048)
    sh["w_ot"] = np.stack([_blockify(np.asarray(inp["w_out"][l], np.float32), 512) for l in range(NL)]).reshape(-1, 2048)
    sh["w_f1"] = np.stack([_blockify(np.asarray(inp["w_ff1"][l], np.float32), 512) for l in range(NL)]).reshape(-1, 2048)
    sh["w_f2"] = np.stack([_blockify(np.asarray(inp["w_ff2"][l], np.float32), 128) for l in range(NL)]).reshape(-1, 2048)
    gvec = np.zeros((128, 4, NL, NCH), np.float32)
    for l in range(NL):
        gvec[:, 0, l, :] = np.asarray(inp["norm_mix_g"][l]).reshape(NCH, 128).T
        gvec[:, 1, l, :] = np.asarray(inp["norm_mlp_g"][l]).reshape(NCH, 128).T
        gvec[:, 2, l, :] = np.asarray(inp["gla_norm_g"][l]).reshape(NCH, 128).T
        gvec[:, 3, l, :] = np.asarray(inp["final_norm_g"]).reshape(NCH, 128).T
    sh["gvec"] = gvec
    sh["bfor"] = np.ascontiguousarray(np.broadcast_to(np.asarray(inp["b_forget"], np.float32)[None], (128, NL, 8)))
    waug = np.zeros((33, NL, 1024), np.float32)
    for l in range(NL):
        waug[0:16, l, :] = np.asarray(inp["w_alpha2"][l])
        waug[32, l, :] = np.asarray(inp["b_alpha"][l])
    sh["waug"] = waug
    s = np.arange(128)[:, None]
    t = np.arange(128)[None, :]
    cst = np.zeros((128, 4, 128), np.float32)
    cst[:, 0, :] = -1.0 * (s <= t)
    cst[:, 1, :] = -1.0 * (s <= t) / 16.0
    cst[:, 2, :] = (s == 63) * np.ones((1, 128))
    cst[:, 3, :] = (s == 127) * np.ones((1, 128))
    sh["cst"] = cst
    cstb = np.zeros((128, 3, 128), np.float32)
    cstb[:, 0, :] = (s <= t)
    cstb[:, 1, :] = (s == t)
    cstb[:, 2, :] = 1.0
    sh["cstb"] = cstb.astype(ml_dtypes.bfloat16)
    padv = np.zeros((128, 1), np.float32)
    padv[:PADT] = 30000.0
    sh["padv"] = padv
    return sh


def prep_x(inp, b):
    x = np.asarray(inp["x"][b], np.float32)
    meta = np.asarray(inp["meta_tokens"], np.float32)
    full = np.concatenate([np.zeros((PADT, D), np.float32), meta, x], axis=0)
    return np.ascontiguousarray(full.T.reshape(NCH, 128, TG))


def kernel(**inputs):
    nc, _ = build()
    sh = prep_shared(inputs)
    in_maps = []
    REAL = {0: 0, 1: 1, 4: 2, 5: 3}
    zero_map = None
    NLAUNCH = 6
    for c in range(NLAUNCH):
        if c in REAL:
            m = dict(sh)
            m["xT"] = prep_x(inputs, REAL[c])
        else:
            if zero_map is None:
                zero_map = {kk: np.zeros_like(vv) for kk, vv in sh.items()}
                zero_map["xT"] = np.zeros((NCH, 128, TG), np.float32)
            m = zero_map
        in_maps.append(m)
    res = run_bass_kernel_spmd(nc, in_maps, core_ids=list(range(NLAUNCH)))
    out = np.empty((4, 4096, D), np.float32)
    for c, b in REAL.items():
        oT = np.asarray(res.results[c]["outT"]).reshape(D, TG)
        out[b] = oT[:, 128:].T
    return out
```
